# Optimizing a Trainium2 kernel written in Bass

```python
import jax, jax.numpy as jnp
from jax import lax
import numpy as np

D_MODEL = 2048
BATCH = 2
SEQ = 4096
DEPTH = 2

N_MIXERS = 2
CHUNK = 128
GMLP_EXPAND = 2
GMLP_WIDTH = GMLP_EXPAND * D_MODEL
GMLP_GROUPS = 16
GMLP_GROUP_DIM = GMLP_WIDTH // GMLP_GROUPS
SB_HEAD_DIM = 128
SB_HEADS = D_MODEL // SB_HEAD_DIM
SB_WIDTH = SB_HEADS * SB_HEAD_DIM
Q_BLOCK = 128
N_A = (DEPTH + 1) // 2
N_B = DEPTH // 2
EPS = 1e-6

kernel_name = "hybrid_gmlp_stickbreaking_trunk"


def rms_norm(x, g):
    xf = x.astype(jnp.float32)
    y = xf * lax.rsqrt(jnp.mean(xf * xf, axis=-1, keepdims=True) + EPS)
    return (y * g.astype(jnp.float32)).astype(x.dtype)


def gmlp_branch(xn, w_in, v_g, w_s, b_s, w_out):
    B, S, _ = xn.shape
    proj = xn @ w_in
    uv = jax.nn.gelu(proj[..., :2 * GMLP_WIDTH])
    zg = proj[..., 2 * GMLP_WIDTH:]
    u, v = jnp.split(uv, 2, axis=-1)
    v = rms_norm(v, v_g)
    vc = v.reshape(B, S // CHUNK, CHUNK, GMLP_GROUPS, GMLP_GROUP_DIM)
    causal = jnp.tril(jnp.ones((CHUNK, CHUNK), dtype=w_s.dtype))
    ws = w_s * causal[None]
    mixed = jnp.einsum('gts,bnsgc->bntgc', ws, vc) + jnp.transpose(b_s)[None, None, :, :, None]
    mixed = mixed.reshape(B, S, GMLP_WIDTH)
    y = u * mixed * jax.nn.silu(zg)
    return y @ w_out


def stick_breaking_attention(q, k, v):
    B, S, H, dh = q.shape
    nb = S // Q_BLOCK
    qh = jnp.transpose(q, (0, 2, 1, 3)) * (dh ** -0.5)
    kh = jnp.transpose(k, (0, 2, 1, 3))
    vh = jnp.transpose(v, (0, 2, 1, 3))
    qb = jnp.transpose(qh.reshape(B, H, nb, Q_BLOCK, dh), (2, 0, 1, 3, 4))
    t0 = jnp.arange(nb, dtype=jnp.int32) * Q_BLOCK
    s_idx = jnp.arange(S, dtype=jnp.int32)[None, :]

    def block(args):
        qblk, start = args
        z = jnp.einsum('bhqd,bhkd->bhqk', qblk, kh).astype(jnp.float32)
        t_idx = start + jnp.arange(Q_BLOCK, dtype=jnp.int32)[:, None]
        mask = s_idx < t_idx
        log_fail = jnp.where(mask, jax.nn.log_sigmoid(-z), 0.0)
        tail = lax.cumsum(log_fail, axis=3, reverse=True) - log_fail
        log_a = jax.nn.log_sigmoid(z) + tail
        a = jnp.where(mask, jnp.exp(log_a), 0.0)
        return jnp.einsum('bhqk,bhkd->bhqd', a.astype(vh.dtype), vh)

    ob = lax.map(block, (qb, t0))
    return jnp.transpose(ob, (1, 0, 3, 2, 4)).reshape(B, S, H, dh)


def stick_breaking_branch(xn, w_in, w_out):
    B, S, _ = xn.shape
    proj = xn @ w_in
    q, k, v, zg = jnp.split(proj, 4, axis=-1)
    shp = (B, S, SB_HEADS, SB_HEAD_DIM)
    o = stick_breaking_attention(q.reshape(shp), k.reshape(shp), v.reshape(shp))
    o = o.reshape(B, S, SB_WIDTH) * jax.nn.silu(zg)
    return o @ w_out


def setup_inputs(seed: int = 0) -> dict:
    key = jax.random.key(seed)
    ks = jax.random.split(key, 12)
    f32 = jnp.float32
    x = jax.random.normal(ks[0], (BATCH, SEQ, D_MODEL), f32)
    norm_g = 1.0 + 0.02 * jax.random.normal(ks[1], (DEPTH, D_MODEL), f32)
    a_w_in = jax.random.normal(ks[2], (N_A, D_MODEL, 3 * GMLP_WIDTH), f32) * D_MODEL ** -0.5
    a_v_norm_g = 1.0 + 0.02 * jax.random.normal(ks[3], (N_A, GMLP_WIDTH), f32)
    a_w_s = jax.random.normal(ks[4], (N_A, GMLP_GROUPS, CHUNK, CHUNK), f32) * (0.5 * CHUNK ** -0.5)
    a_b_s = 1.0 + 0.1 * jax.random.normal(ks[5], (N_A, GMLP_GROUPS, CHUNK), f32)
    a_w_out = jax.random.normal(ks[6], (N_A, GMLP_WIDTH, D_MODEL), f32) * GMLP_WIDTH ** -0.5
    b_w_in = jax.random.normal(ks[7], (N_B, D_MODEL, 4 * SB_WIDTH), f32) * D_MODEL ** -0.5
    b_w_out = jax.random.normal(ks[8], (N_B, SB_WIDTH, D_MODEL), f32) * SB_WIDTH ** -0.5
    final_g = 1.0 + 0.02 * jax.random.normal(ks[9], (D_MODEL,), f32)
    return {"x": x, "norm_g": norm_g, "a_w_in": a_w_in, "a_v_norm_g": a_v_norm_g,
            "a_w_s": a_w_s, "a_b_s": a_b_s, "a_w_out": a_w_out,
            "b_w_in": b_w_in, "b_w_out": b_w_out, "final_g": final_g}


def reference(x, norm_g, a_w_in, a_v_norm_g, a_w_s, a_b_s, a_w_out, b_w_in, b_w_out, final_g):
    h = x
    for i in range(DEPTH):
        hn = rms_norm(h, norm_g[i])
        j = i // N_MIXERS
        if i % N_MIXERS == 0:
            y = gmlp_branch(hn, a_w_in[j], a_v_norm_g[j], a_w_s[j], a_b_s[j], a_w_out[j])
        else:
            y = stick_breaking_branch(hn, b_w_in[j], b_w_out[j])
        h = h + y
    return rms_norm(h, final_g)
```

```python
import numpy as np
import ml_dtypes
import concourse.bass as bass
import concourse.mybir as mybir
from concourse.bass_utils import run_bass_kernel_spmd

F32 = mybir.dt.float32
BF16 = mybir.dt.bfloat16
AF = mybir.ActivationFunctionType
ALU = mybir.AluOpType
AX = mybir.AxisListType

P = 128
T = 1024
D = 2048
KT = D // P
NCH = T // P
E = 4096
G = 16
H = 16
DH = 128
S = 4096
WB = 256
EPS = 1e-6
NCORES = 8
ENGS = ("pe", "act", "dve", "pool", "sp")


class Op:
    __slots__ = ("eng", "fn", "deps", "ms", "val", "dma", "sem", "idx")


class Res:
    def __init__(self):
        self.w = None
        self.r = {}
        self.rd = []

    def deps_r(self):
        return [self.w] if self.w is not None else []

    def deps_w(self):
        d = list(self.r.values()) + list(self.rd)
        if self.w is not None:
            d.append(self.w)
        return d

    def set_w(self, op):
        self.w = op
        self.r = {}
        self.rd = []

    def add_r(self, op):
        if op.dma:
            self.rd.append(op)
        else:
            self.r[op.eng] = op


class Sched:
    NRING = 8

    def __init__(self):
        self.q = {e: [] for e in ENGS}

    def _mk(self, eng, fn, deps, dma):
        o = Op()
        o.eng = eng
        o.fn = fn
        o.deps = [d for d in deps if d is not None]
        o.ms = False
        o.val = None
        o.dma = dma
        o.sem = None
        o.idx = len(self.q[eng])
        self.q[eng].append(o)
        return o

    def do(self, eng, fn, reads=(), writes=(), deps=(), dma=False):
        dl = list(deps)
        for r in reads:
            dl += r.deps_r()
        for w in writes:
            dl += w.deps_w()
        o = self._mk(eng, fn, dl, dma)
        for r in reads:
            r.add_r(o)
        for w in writes:
            w.set_w(o)
        return o

    def emit(self, nc, sems, dma_sems):
        for e in ENGS:
            for o in self.q[e]:
                for d in o.deps:
                    if d.dma:
                        continue
                    if d.eng == o.eng and o.eng == "pe" and not o.dma:
                        continue
                    d.ms = True
        for e in ENGS:
            c = 0
            nd = 0
            for o in self.q[e]:
                if o.dma:
                    o.sem = dma_sems[e][nd % self.NRING]
                    o.val = 16 * (nd // self.NRING + 1)
                    nd += 1
                elif o.ms:
                    c += 1
                    o.val = c
        engobj = {"pe": "tensor", "act": "scalar", "dve": "vector", "pool": "gpsimd", "sp": "sync"}

        def run(e, eng):
            waited = {}

            def wait(sem, val):
                k = id(sem)
                if waited.get(k, 0) < val:
                    eng.wait_ge(sem, val)
                    waited[k] = val

            for o in self.q[e]:
                for d in o.deps:
                    if d.dma:
                        wait(d.sem, d.val)
                    else:
                        if d.eng == e and e == "pe" and not o.dma:
                            continue
                        wait(sems[d.eng], d.val)
                if o.dma:
                    if o.val > 16:
                        wait(o.sem, o.val - 16)
                    ins = o.fn(eng)
                    ins.then_inc(o.sem, 16)
                else:
                    ins = o.fn(eng)
                    if o.ms:
                        ins.then_inc(sems[e], 1)
            last = {}
            for o in self.q[e]:
                if o.dma:
                    last[id(o.sem)] = (o.sem, o.val)
            for sem, val in last.values():
                wait(sem, val)

        with nc.Block() as block:
            @block.tensor
            def _(eng):
                run("pe", eng)

            @block.scalar
            def _(eng):
                run("act", eng)

            @block.vector
            def _(eng):
                run("dve", eng)

            @block.gpsimd
            def _(eng):
                run("pool", eng)

            @block.sync
            def _(eng):
                run("sp", eng)


class Ctx:
    def __init__(self, nc):
        self.nc = nc
        self.S = Sched()
        self._stack = []

    def sb(self, name, shape, dt):
        cm = self.nc.sbuf_tensor(name, shape, dt)
        t = cm.__enter__()
        self._stack.append(cm)
        return t

    def ps(self, name, shape, dt):
        cm = self.nc.psum_tensor(name, shape, dt)
        t = cm.__enter__()
        self._stack.append(cm)
        return t

    def sem(self, name):
        cm = self.nc.semaphore(name)
        t = cm.__enter__()
        self._stack.append(cm)
        return t

    def finish(self):
        sems = {e: self.sem("s_" + e) for e in ENGS}
        dma_sems = {e: [self.sem(f"d_{e}{i}") for i in range(Sched.NRING)] for e in ("sp", "act", "pool")}
        dma_sems["pe"] = dma_sems["sp"]
        dma_sems["dve"] = dma_sems["sp"]
        self.S.emit(self.nc, sems, dma_sems)
        while self._stack:
            self._stack.pop().__exit__(None, None, None)


def dram_in(nc, name, shape, dt):
    return nc.dram_tensor(name, list(shape), dt, kind="ExternalInput").ap()


def dram_out(nc, name, shape, dt):
    return nc.dram_tensor(name, list(shape), dt, kind="ExternalOutput").ap()


class WStream:
    def __init__(self, cx, blocks, nslots=4, nstage=3, ahead=2):
        self.cx = cx
        self.slots = [cx.sb(f"wslot{i}", [P, KT, WB], BF16) for i in range(nslots)]
        self.sres = [Res() for _ in range(nslots)]
        self.stage = [cx.sb(f"wstage{i}", [P, 4, WB], F32) for i in range(nstage)]
        self.gres = [Res() for _ in range(nstage)]
        self.blocks = blocks
        self.ahead = ahead
        self.issued = 0
        self.taken = 0
        self.ng = 0
        self.ready = []

    def _issue(self):
        w_ap, r0, c0 = self.blocks[self.issued]
        S = self.cx.S
        si = self.issued % len(self.slots)
        self.issued += 1
        slot = self.slots[si]
        sres = self.sres[si]
        src = w_ap[r0:r0 + KT * P, c0:c0 + WB].rearrange("(kt p) n -> p kt n", p=P)
        wdeps = sres.deps_w()
        last = None
        for qd in range(4):
            gi = self.ng % len(self.stage)
            self.ng += 1
            st = self.stage[gi]
            gres = self.gres[gi]
            S.do("sp", (lambda e, st=st, qd=qd, src=src: e.dma_start(out=st[:, :, :], in_=src[:, qd * 4:(qd + 1) * 4, :])),
                 writes=[gres], dma=True)
            last = S.do("pool", (lambda e, st=st, qd=qd, slot=slot: e.tensor_copy(out=slot[:, qd * 4:(qd + 1) * 4, :], in_=st[:, :, :])),
                        reads=[gres], deps=(wdeps if qd == 0 else []))
        sres.set_w(last)
        self.ready.append((slot, sres))

    def get(self, w_ap=None, r0=None, c0=None):
        if w_ap is not None:
            assert self.blocks[self.taken][1:] == (r0, c0), (self.blocks[self.taken], r0, c0)
        while self.issued < len(self.blocks) and self.issued <= self.taken + self.ahead:
            self._issue()
        r = self.ready[self.taken]
        self.taken += 1
        return r

    load = get


class Banks:
    def __init__(self, cx, n, name, shape=(P, 512), dt=F32):
        self.t = [cx.ps(f"{name}{i}", list(shape), dt) for i in range(n)]
        self.res = [Res() for _ in range(n)]
        self.i = 0

    def next(self):
        k = self.i % len(self.t)
        self.i += 1
        return self.t[k], self.res[k]


class Ring:
    def __init__(self, cx, n, name, shape, dt):
        self.t = [cx.sb(f"{name}{i}", list(shape), dt) for i in range(n)]
        self.res = [Res() for _ in range(n)]
        self.i = 0

    def next(self):
        k = self.i % len(self.t)
        self.i += 1
        return self.t[k], self.res[k]


def mm_group(S, bank_ap, bres, pairs, reads, extra_deps=()):
    n = len(pairs)
    last = None
    dl = list(extra_deps) + bres.deps_w()
    for r in reads:
        dl += r.deps_r()
    for i, (l, r) in enumerate(pairs):
        if i == 0:
            last = S.do("pe", (lambda e, l=l, r=r: e.matmul(bank_ap, l, r, start=True, stop=(n == 1))), deps=dl)
        else:
            last = S.do("pe", (lambda e, l=l, r=r, i=i: e.matmul(bank_ap, l, r, start=False, stop=(i == n - 1))))
    for r in reads:
        r.add_r(last)
    bres.set_w(last)
    return last


_UID = [0]


def uid(p):
    _UID[0] += 1
    return f"{p}{_UID[0]}"


def rms_hnT(cx, src_dram, g_bc, g_res, hnT, hnT_res, ident_bf, ident_res, xring, hnring, trbanks, epsc,
            ssq_in=None, load_deps=None):
    S = cx.S
    stat = cx.sb(uid("stat"), [P, 4 * NCH], F32)
    for ch in range(NCH):
        xc, xres = xring.next()
        S.do("sp", (lambda e, xc=xc, ch=ch: e.dma_start(out=xc[:, :], in_=src_dram[ch * P:(ch + 1) * P, :])),
             writes=[xres], deps=(load_deps[ch] if load_deps else []), dma=True)
        ssq = stat[:, ch:ch + 1]
        if ssq_in is None:
            hb, hres = hnring.next()
            o1 = S.do("act", (lambda e, xc=xc, hb=hb, ssq=ssq: e.activation(out=hb[:, :], in_=xc[:, :], func=AF.Square, accum_out=ssq)),
                      reads=[xres], writes=[hres])
        else:
            st_t, st_last, npart = ssq_in
            o1 = S.do("dve", (lambda e, ssq=ssq, ch=ch, st_t=st_t, npart=npart: e.reduce_sum(out=ssq, in_=st_t[:, ch * npart:(ch + 1) * npart], axis=AX.X)),
                      deps=[st_last])
        sq = stat[:, NCH + ch:NCH + ch + 1]
        rs = stat[:, 2 * NCH + ch:2 * NCH + ch + 1]
        o2 = S.do("act", (lambda e, ssq=ssq, sq=sq: e.activation(out=sq, in_=ssq, func=AF.Sqrt, scale=1.0 / D, bias=epsc)), deps=[o1])
        o3 = S.do("dve", (lambda e, sq=sq, rs=rs: e.reciprocal(out=rs, in_=sq)), deps=[o2])
        hb, hres = hnring.next()
        S.do("dve", (lambda e, xc=xc, hb=hb, rs=rs: e.scalar_tensor_tensor(out=hb[:, :], in0=xc[:, :], scalar=rs, in1=g_bc[:, :],
                                                                            op0=ALU.mult, op1=ALU.mult)),
             reads=[xres, g_res], writes=[hres], deps=[o3])
        for k4 in range(KT // 4):
            tb, tres = trbanks.next()
            lastt = None
            for kk in range(4):
                k = k4 * 4 + kk
                dl = hres.deps_r() + ident_res.deps_r() + (tres.deps_w() if kk == 0 else [])
                lastt = S.do("pe", (lambda e, tb=tb, hb=hb, k=k, kk=kk: e.transpose(out=tb[:, kk * P:(kk + 1) * P], in_=hb[:, k * P:(k + 1) * P],
                                                                              identity=ident_bf[:, :])), deps=dl)
            hres.add_r(lastt)
            tres.set_w(lastt)
            S.do("dve", (lambda e, tb=tb, k4=k4, ch=ch: e.tensor_copy(
                out=hnT[:, k4 * 4:(k4 + 1) * 4, ch * P:(ch + 1) * P],
                in_=tb[:, 0:4 * P].rearrange("p (a b) -> p a b", a=4))),
                 reads=[tres], writes=[hnT_res[ch]])


def make_consts(cx, ident_d):
    S = cx.S
    epst = cx.sb("epst", [P, 1], F32)
    S.do("pool", (lambda e: e.memset(epst[:, :], EPS)))
    idf = cx.sb("idf", [P, P], F32)
    idb = cx.sb("idb", [P, P], BF16)
    r1 = Res()
    r2 = Res()
    S.do("sp", (lambda e: e.dma_start(out=idf[:, :], in_=ident_d[:, :])), writes=[r1], dma=True)
    S.do("dve", (lambda e: e.tensor_copy(out=idb[:, :], in_=idf[:, :])), reads=[r1], writes=[r2])
    return epst[:, 0:1], idb, r2


class Deferred:
    def __init__(self, lag=1):
        self.q = []
        self.lag = lag

    def push(self, fn):
        self.q.append(fn)

    def tick(self):
        while len(self.q) > self.lag:
            self.q.pop(0)()

    def flush(self):
        while self.q:
            self.q.pop(0)()


def build_A():
    nc = bass.Bass("TRN2", target_bir_lowering=False)
    x = dram_in(nc, "x", [T, D], F32)
    g0 = dram_in(nc, "g0", [1, D], F32)
    g1 = dram_in(nc, "g1", [1, D], F32)
    w_in0 = dram_in(nc, "w_in0", [D, 3 * E], F32)
    vng = dram_in(nc, "vng", [P, E // P], F32)
    wsT_d = dram_in(nc, "wsT", [P, G * P], F32)
    bs_d = dram_in(nc, "bs", [1, G * P], F32)
    cmask_d = dram_in(nc, "cmask", [P, P], F32)
    ident_d = dram_in(nc, "ident", [P, P], F32)
    w_out0 = dram_in(nc, "w_out0", [E, D], F32)
    w_in1 = dram_in(nc, "w_in1", [D, 4 * D], F32)
    h1 = dram_out(nc, "h1", [T, D], F32)
    qT_o = dram_out(nc, "qT", [H, P, T], BF16)
    kT_o = dram_out(nc, "kT", [H, P, T], BF16)
    sgT_o = dram_out(nc, "sgT", [H, P, T], BF16)
    v_o = dram_out(nc, "vv", [H, P, NCH, DH], BF16)

    cx = Ctx(nc)
    S = cx.S
    epsc, ident_bf, ident_res = make_consts(cx, ident_d)

    blocks = []
    for cb in range(E // WB):
        blocks.append((w_in0, 0, E + cb * WB))
    for cb in range(E // WB):
        blocks.append((w_in0, 0, cb * WB))
        blocks.append((w_in0, 0, 2 * E + cb * WB))
    for db in range(D // WB):
        blocks.append((w_out0, 0, db * WB))
        blocks.append((w_out0, KT * P, db * WB))
    for col0 in (D, 2 * D, 0, 3 * D):
        for cb in range(D // WB):
            blocks.append((w_in1, 0, col0 + cb * WB))

    hnT = cx.sb("hnT", [P, KT, T], BF16)
    hnT_res = [Res() for _ in range(NCH)]
    V = cx.sb("V", [P, NCH, E], BF16)
    g_bc = cx.sb("g_bc", [P, D], F32)
    g_res = Res()
    xring = Ring(cx, 1, "xc", [P, D], F32)
    hnring = Ring(cx, 2, "hnb", [P, D], BF16)
    mmb = Banks(cx, 4, "mm")
    mixb = Banks(cx, 1, "mix", shape=(P, 1024))
    trb = Banks(cx, 2, "tr", shape=(P, 1024), dt=BF16)
    ws = WStream(cx, blocks, nslots=4, nstage=3, ahead=2)

    S.do("sp", (lambda e: e.dma_start(out=g_bc[:, :], in_=g0[0:1, :].partition_broadcast(P))), writes=[g_res], dma=True)
    rms_hnT(cx, x, g_bc, g_res, hnT, hnT_res, ident_bf, ident_res, xring, hnring, trb, epsc)

    gv = cx.sb("gv", [P, E // P], F32)
    gv_res = Res()
    S.do("sp", (lambda e: e.dma_start(out=gv[:, :], in_=vng[:, :])), writes=[gv_res], dma=True)
    cm = cx.sb("cm", [P, P], F32)
    cm_res = Res()
    S.do("sp", (lambda e: e.dma_start(out=cm[:, :], in_=cmask_d[:, :])), writes=[cm_res], dma=True)
    wsf, wsf_res = xring.next()
    S.do("sp", (lambda e: e.dma_start(out=wsf[:, :], in_=wsT_d[:, :])), writes=[wsf_res], dma=True)
    wsb = cx.sb("wsb", [P, G * P], BF16)
    wsb_res = Res()
    for g in range(G):
        S.do("dve", (lambda e, g=g: e.tensor_tensor(out=wsb[:, g * P:(g + 1) * P], in0=wsf[:, g * P:(g + 1) * P], in1=cm[:, :], op=ALU.mult)),
             reads=[wsf_res, cm_res], writes=[wsb_res])
    bias_bc = g_bc
    S.do("sp", (lambda e: e.dma_start(out=bias_bc[:, :], in_=bs_d[0:1, :].partition_broadcast(P))), writes=[g_res], dma=True)

    vss = cx.sb("vss", [P, NCH * 16], F32)
    junk = cx.sb("junk", [P, 512], BF16)
    lastsq = None
    for cb in range(E // WB):
        slot, sres = ws.get(w_in0, 0, E + cb * WB)
        for ch in range(NCH):
            bank, bres = mmb.next()
            mm_group(S, bank[:, 0:WB], bres,
                     [(hnT[:, k, ch * P:(ch + 1) * P], slot[:, k, :]) for k in range(KT)],
                     reads=[sres, hnT_res[ch]])
            vsl = V[:, ch, cb * WB:(cb + 1) * WB]
            og_ = S.do("act", (lambda e, bank=bank, vsl=vsl: e.activation(out=vsl, in_=bank[:, 0:WB], func=AF.Gelu_apprx_tanh)),
                       reads=[bres])
            acc = vss[:, ch * 16 + cb:ch * 16 + cb + 1]
            lastsq = S.do("act", (lambda e, vsl=vsl, acc=acc: e.activation(out=junk[:, 0:WB], in_=vsl, func=AF.Square, accum_out=acc)),
                          deps=[og_])
    vst = cx.sb("vst", [P, 3 * NCH], F32)
    o1 = None
    for ch in range(NCH):
        o1 = S.do("dve", (lambda e, ch=ch: e.reduce_sum(out=vst[:, ch:ch + 1], in_=vss[:, ch * 16:(ch + 1) * 16], axis=AX.X)), deps=[lastsq])
    o2 = S.do("act", (lambda e: e.activation(out=vst[:, NCH:2 * NCH], in_=vst[:, 0:NCH], func=AF.Sqrt, scale=1.0 / E, bias=epsc)), deps=[o1])
    o3 = S.do("dve", (lambda e: e.reciprocal(out=vst[:, 2 * NCH:3 * NCH], in_=vst[:, NCH:2 * NCH])), deps=[o2])
    vdeps = {}
    for ch in range(NCH):
        eng = "dve" if ch % 2 == 0 else "pool"
        vdeps[eng] = S.do(eng, (lambda e, ch=ch: e.tensor_scalar(out=V[:, ch, :], in0=V[:, ch, :], scalar1=vst[:, 2 * NCH + ch:2 * NCH + ch + 1],
                                                                 scalar2=None, op0=ALU.mult)), deps=[o3])
    vdeps = list(vdeps.values())
    VR = [Res() for _ in range(E // P)]

    uring = Ring(cx, 2, "u_sb", [P, T], BF16)
    sgring = Ring(cx, 2, "sg_sb", [P, T], BF16)
    tmring = Ring(cx, 2, "tmp", [P, T], F32)
    usring = Ring(cx, 2, "us", [P, T], BF16)
    for cb in range(E // WB):
        slot_u, sres_u = ws.get(w_in0, 0, cb * WB)
        slot_z, sres_z = ws.get(w_in0, 0, 2 * E + cb * WB)
        for cti in range(WB // P):
            ct = cb * (WB // P) + cti
            g = ct // 2
            u_sb, ures = uring.next()
            sg_sb, sgres = sgring.next()
            for (slot, sres, dst, dres, fn) in ((slot_u, sres_u, u_sb, ures, AF.Gelu_apprx_tanh), (slot_z, sres_z, sg_sb, sgres, AF.Silu)):
                for half in range(2):
                    bank, bres = mmb.next()
                    mm_group(S, bank[:, :], bres,
                             [(slot[:, k, cti * P:(cti + 1) * P], hnT[:, k, half * 512:(half + 1) * 512]) for k in range(KT)],
                             reads=[sres] + hnT_res[half * 4:(half + 1) * 4])
                    S.do("act", (lambda e, bank=bank, dst=dst, half=half, fn=fn: e.activation(out=dst[:, half * 512:(half + 1) * 512], in_=bank[:, :], func=fn)),
                         reads=[bres], writes=[dres])
            mixp, mres = mixb.next()
            lastm = None
            for ch in range(NCH):
                dl = (vdeps + wsb_res.deps_r() + mres.deps_w()) if ch == 0 else []
                lastm = S.do("pe", (lambda e, mixp=mixp, ch=ch, ct=ct, g=g: e.matmul(mixp[:, ch * P:(ch + 1) * P], V[:, ch, ct * P:(ct + 1) * P],
                                                                             wsb[:, g * P:(g + 1) * P], start=True, stop=True)), deps=dl)
            mres.set_w(lastm)
            tmp, tres = tmring.next()
            S.do("dve", (lambda e, mixp=mixp, tmp=tmp, ct=ct, g=g: e.scalar_tensor_tensor(
                out=tmp[:, :].rearrange("p (a b) -> p a b", a=NCH), in0=mixp[:, :].rearrange("p (a b) -> p a b", a=NCH),
                scalar=gv[:, ct:ct + 1], in1=bias_bc[:, g * P:(g + 1) * P].unsqueeze(1).to_broadcast([P, NCH, P]),
                op0=ALU.mult, op1=ALU.add)),
                 reads=[mres, gv_res, g_res], writes=[tres])
            us, usres = usring.next()
            S.do("pool", (lambda e, us=us, u_sb=u_sb, sg_sb=sg_sb: e.tensor_tensor(out=us[:, :], in0=u_sb[:, :], in1=sg_sb[:, :], op=ALU.mult)),
                 reads=[ures, sgres], writes=[usres])
            S.do("dve", (lambda e, tmp=tmp, us=us, ct=ct: e.tensor_tensor(
                out=V[:, :, ct * P:(ct + 1) * P], in0=tmp[:, :].rearrange("p (a b) -> p a b", a=NCH),
                in1=us[:, :].rearrange("p (a b) -> p a b", a=NCH), op=ALU.mult)),
                 reads=[tres, usres], writes=[VR[ct]], deps=[lastm])

    hss = cx.sb("hss", [P, NCH * 8], F32)
    xqring = Ring(cx, 3, "xq", [P, WB], F32)
    hpring = Ring(cx, 3, "h1p", [P, WB], F32)
    h1w = {ch: [] for ch in range(NCH)}
    dfr = Deferred(lag=1)
    lasth = None
    for db in range(D // WB):
        slotA, sresA = ws.get(w_out0, 0, db * WB)
        slotB, sresB = ws.get(w_out0, KT * P, db * WB)
        for ch in range(NCH):
            xq, xqres = xqring.next()
            S.do("sp", (lambda e, xq=xq, ch=ch, db=db: e.dma_start(out=xq[:, :], in_=x[ch * P:(ch + 1) * P, db * WB:(db + 1) * WB])),
                 writes=[xqres], dma=True)
            dfr.tick()
            bank, bres = mmb.next()
            pairs = [(V[:, ch, ct * P:(ct + 1) * P], (slotA if ct < KT else slotB)[:, ct % KT, :]) for ct in range(E // P)]
            mm_group(S, bank[:, 0:WB], bres, pairs, reads=[sresA, sresB] + VR)
            hp, hpres = hpring.next()
            S.do("dve", (lambda e, bank=bank, hp=hp, xq=xq: e.tensor_tensor(out=hp[:, :], in0=bank[:, 0:WB], in1=xq[:, :], op=ALU.add)),
                 reads=[bres, xqres], writes=[hpres])
            acc = hss[:, ch * 8 + db:ch * 8 + db + 1]
            lasth = S.do("act", (lambda e, hp=hp, acc=acc: e.activation(out=junk[:, 0:WB], in_=hp[:, :], func=AF.Square, accum_out=acc)),
                         reads=[hpres])

            def store(hp=hp, hpres=hpres, ch=ch, db=db):
                o = S.do("sp", (lambda e: e.dma_start(out=h1[ch * P:(ch + 1) * P, db * WB:(db + 1) * WB], in_=hp[:, :])),
                         reads=[hpres], dma=True)
                h1w[ch].append(o)
            dfr.push(store)
    dfr.flush()

    S.do("sp", (lambda e: e.dma_start(out=g_bc[:, :], in_=g1[0:1, :].partition_broadcast(P))), writes=[g_res], dma=True)
    rms_hnT(cx, h1, g_bc, g_res, hnT, hnT_res, ident_bf, ident_res, xring, hnring, trb, epsc,
            ssq_in=(hss, lasth, 8), load_deps=h1w)

    evring = Ring(cx, 3, "ev", [P, T], BF16)
    v1ring = Ring(cx, 3, "v1ev", [P, WB], BF16)
    scale = float(DH) ** -0.5
    dfr = Deferred(lag=1)

    def fm_part(col0, dst, kind):
        for cb in range(D // WB):
            slot, sres = ws.get(w_in1, 0, col0 + cb * WB)
            for cti in range(WB // P):
                hd = cb * (WB // P) + cti
                ev, evres = evring.next()
                for half in range(2):
                    bank, bres = mmb.next()
                    mm_group(S, bank[:, :], bres,
                             [(slot[:, k, cti * P:(cti + 1) * P], hnT[:, k, half * 512:(half + 1) * 512]) for k in range(KT)],
                             reads=[sres] + hnT_res[half * 4:(half + 1) * 4])
                    dsl = ev[:, half * 512:(half + 1) * 512]
                    if kind == "k":
                        S.do("dve", (lambda e, bank=bank, dsl=dsl: e.tensor_copy(out=dsl, in_=bank[:, :])), reads=[bres], writes=[evres])
                    elif kind == "q":
                        S.do("act", (lambda e, bank=bank, dsl=dsl: e.activation(out=dsl, in_=bank[:, :], func=AF.Copy, scale=scale)),
                             reads=[bres], writes=[evres])
                    else:
                        S.do("act", (lambda e, bank=bank, dsl=dsl: e.activation(out=dsl, in_=bank[:, :], func=AF.Silu)),
                             reads=[bres], writes=[evres])
                dfr.tick()
                dfr.push(lambda ev=ev, evres=evres, hd=hd, dst=dst: S.do(
                    "sp", (lambda e: e.dma_start(out=dst[hd, :, :], in_=ev[:, :])), reads=[evres], dma=True))

    fm_part(D, kT_o, "k")
    for cb in range(D // WB):
        slot, sres = ws.get(w_in1, 0, 2 * D + cb * WB)
        for ch in range(NCH):
            bank, bres = mmb.next()
            mm_group(S, bank[:, 0:WB], bres,
                     [(hnT[:, k, ch * P:(ch + 1) * P], slot[:, k, :]) for k in range(KT)],
                     reads=[sres, hnT_res[ch]])
            ve, veres = v1ring.next()
            S.do("dve", (lambda e, bank=bank, ve=ve: e.tensor_copy(out=ve[:, :], in_=bank[:, 0:WB])), reads=[bres], writes=[veres])
            dfr.tick()
            dfr.push(lambda ve=ve, veres=veres, cb=cb, ch=ch: S.do("sp", (lambda e: e.dma_start(
                out=v_o[cb * 2:(cb + 1) * 2, :, ch, :].rearrange("h p d -> p h d"),
                in_=ve[:, :].rearrange("p (h d) -> p h d", h=2))), reads=[veres], dma=True))
    fm_part(0, qT_o, "q")
    fm_part(3 * D, sgT_o, "g")
    dfr.flush()

    cx.finish()
    return nc


def build_B():
    NHL = 4
    nc = bass.Bass("TRN2", target_bir_lowering=False)
    qT_d = dram_in(nc, "qT", [NHL, P, S], BF16)
    kT_d = dram_in(nc, "kT", [NHL, P, S], BF16)
    v_d = dram_in(nc, "vv", [NHL, P, S // P, DH], BF16)
    tri_d = dram_in(nc, "tri", [P, 3 * P], F32)
    oT_d = dram_out(nc, "oT", [NHL, P, S], BF16)

    cx = Ctx(nc)
    S_ = cx.S
    trif = cx.sb("trif", [P, 3 * P], F32)
    trif_res = Res()
    S_.do("sp", (lambda e: e.dma_start(out=trif[:, :], in_=tri_d[:, :])), writes=[trif_res], dma=True)
    trib = cx.sb("trib", [P, 3 * P], BF16)
    trib_res = Res()
    S_.do("dve", (lambda e: e.tensor_copy(out=trib[:, :], in_=trif[:, :])), reads=[trif_res], writes=[trib_res])
    zer = cx.sb("zer", [P, 512], BF16)
    zer_res = Res()
    S_.do("pool", (lambda e: e.memset(zer[:, :], 0.0)), writes=[zer_res])
    negU = trib[:, 0:P]
    negR = trib[:, P:2 * P]
    mask_b = trib[:, 2 * P:3 * P]
    mask_f = trif[:, 2 * P:3 * P]

    qring = Ring(cx, 2, "q_sb", [P, S], BF16)
    kring = Ring(cx, 2, "k_sb", [P, S], BF16)
    vring = Ring(cx, 2, "v_sb", [P, S // P, DH], BF16)
    oring = Ring(cx, 2, "o_sb", [P, S], BF16)
    zb = Banks(cx, 2, "z")
    pb = Banks(cx, 2, "pp")
    ob = Banks(cx, 2, "oo")
    ering = Ring(cx, 4, "e_sb", [P, 512], F32)
    spring = Ring(cx, 4, "sp_sb", [P, 512], BF16)
    wring = Ring(cx, 2, "w_sb", [P, 512], F32)
    aring = Ring(cx, 3, "a_sb", [P, 512], BF16)

    items = []
    first_item = {}
    heads = []
    for hh in range(NHL):
        heads.append(dict(q=qring.next(), k=kring.next(), v=vring.next(), o=oring.next(), lq=None, lk=None, lv=None))
        first_item[hh] = len(items)
        for tb in range(S // 512):
            for kb in range(4 * tb + 3, -1, -1):
                items.append((hh, tb, kb))

    loaded = set()

    def ensure_loaded(hh):
        if hh in loaded or hh >= NHL:
            return
        loaded.add(hh)
        hd = heads[hh]
        for key, src, nm in (("k", kT_d, "lk"), ("q", qT_d, "lq"), ("v", v_d, "lv")):
            t, res = hd[key]
            wd = res.deps_w()
            ops = []
            for c in range(4):
                if key == "v":
                    fn = (lambda e, c=c, t=t: e.dma_start(out=t[:, c * 8:(c + 1) * 8, :], in_=v_d[hh, :, c * 8:(c + 1) * 8, :]))
                else:
                    fn = (lambda e, c=c, t=t, src=src: e.dma_start(out=t[:, c * 1024:(c + 1) * 1024], in_=src[hh, :, c * 1024:(c + 1) * 1024]))
                ops.append(S_.do("sp", fn, deps=wd, dma=True))
            res.set_w(ops[-1])
            hd[nm] = ops

    st1 = {}

    def stage1a(n):
        hh, tb, kb = items[n]
        ensure_loaded(hh)
        hd = heads[hh]
        q_sb, qres = hd["q"]
        k_sb, kres = hd["k"]
        c0 = P * (kb - 4 * tb) if kb >= 4 * tb else 0
        zt, zres = zb.next()
        z = S_.do("pe", (lambda e: e.matmul(zt[:, c0:512], k_sb[:, kb * P:(kb + 1) * P], q_sb[:, tb * 512 + c0:(tb + 1) * 512], start=True, stop=True)),
                  deps=hd["lk"] + hd["lq"] + zres.deps_w())
        zres.set_w(z)
        qres.add_r(z)
        kres.add_r(z)
        e_sb, eres = ering.next()
        S_.do("act", (lambda e: e.activation(out=e_sb[:, c0:512], in_=zt[:, c0:512], func=AF.Exp)), reads=[zres], writes=[eres])
        st1[n] = (e_sb, eres, c0)

    def stage1b(n):
        hh, tb, kb = items[n]
        e_sb, eres, c0 = st1[n]
        sp_sb, spres = spring.next()
        S_.do("act", (lambda e: e.activation(out=sp_sb[:, c0:512], in_=e_sb[:, c0:512], func=AF.Ln, bias=1.0)), reads=[eres], writes=[spres])
        if kb >= 4 * tb:
            S_.do("pool", (lambda e: e.tensor_tensor(out=sp_sb[:, c0:c0 + P], in0=sp_sb[:, c0:c0 + P], in1=mask_b, op=ALU.mult)),
                  reads=[trib_res], writes=[spres])
            S_.do("pool", (lambda e: e.tensor_tensor(out=e_sb[:, c0:c0 + P], in0=e_sb[:, c0:c0 + P], in1=mask_f, op=ALU.mult)),
                  reads=[trif_res], writes=[eres])
        st1[n] = (e_sb, eres, sp_sb, spres, c0)

    sweep = {}

    def stage2(n):
        hh, tb, kb = items[n]
        hd = heads[hh]
        v_sb, vres = hd["v"]
        o_sb, ores = hd["o"]
        e_sb, eres, sp_sb, spres, c0 = st1.pop(n)
        first = kb == 4 * tb + 3
        last = kb == 0
        if first:
            pt, pres = pb.next()
            ot, otres = ob.next()
            sweep["p"] = (pt, pres)
            sweep["o"] = (ot, otres)
            zp = S_.do("pe", (lambda e: e.matmul(pt[:, :], zer[:, 0:P], zer[:, :], start=True, stop=False)),
                       deps=zer_res.deps_r() + pres.deps_w())
            pres.set_w(zp)
            zo = S_.do("pe", (lambda e: e.matmul(ot[:, :], zer[:, 0:P], zer[:, :], start=True, stop=False)),
                       deps=zer_res.deps_r() + otres.deps_w())
            otres.set_w(zo)
        pt, pres = sweep["p"]
        ot, otres = sweep["o"]
        t1 = S_.do("pe", (lambda e: e.matmul(pt[:, c0:512], negU, sp_sb[:, c0:512], start=False, stop=False)),
                   deps=spres.deps_r() + trib_res.deps_r() + pres.deps_w())
        spres.add_r(t1)
        pres.set_w(t1)
        w_sb, wres = wring.next()
        S_.do("act", (lambda e: e.activation(out=w_sb[:, c0:512], in_=pt[:, c0:512], func=AF.Exp)), reads=[pres], writes=[wres])
        a_sb, ares = aring.next()
        S_.do("dve", (lambda e: e.tensor_tensor(out=a_sb[:, c0:512], in0=e_sb[:, c0:512], in1=w_sb[:, c0:512], op=ALU.mult)),
              reads=[eres, wres], writes=[ares])

        def av():
            o = S_.do("pe", (lambda e: e.matmul(ot[:, c0:512], v_sb[:, kb, :], a_sb[:, c0:512], start=False, stop=last)),
                      deps=ares.deps_r() + hd["lv"] + otres.deps_w())
            ares.add_r(o)
            vres.add_r(o)
            otres.set_w(o)
            if last:
                S_.do("dve", (lambda e: e.tensor_copy(out=o_sb[:, tb * 512:(tb + 1) * 512], in_=ot[:, :])), reads=[otres], writes=[ores])
                if tb == S // 512 - 1:
                    S_.do("sp", (lambda e: e.dma_start(out=oT_d[hh, :, :], in_=o_sb[:, :])), reads=[ores], dma=True)

        def rest():
            if not last:
                r = S_.do("pe", (lambda e: e.matmul(pt[:, c0:512], negR, sp_sb[:, c0:512], start=False, stop=False)),
                          deps=pres.deps_w())
                spres.add_r(r)
                pres.set_w(r)

        return av, rest

    LA = 2
    N = len(items)
    for n in range(min(LA, N)):
        stage1a(n)
        stage1b(n)
    prev_av = None
    for n in range(N):
        hh = items[n][0]
        if n == first_item[hh] + 12:
            ensure_loaded(hh + 1)
        if n + LA < N:
            stage1a(n + LA)
        av, rest = stage2(n)
        if n + LA < N:
            stage1b(n + LA)
        if prev_av is not None:
            prev_av()
        rest()
        prev_av = av
    prev_av()

    cx.finish()
    return nc


def build_C():
    nc = bass.Bass("TRN2", target_bir_lowering=False)
    oT_d = dram_in(nc, "oT", [H, P, T], BF16)
    sgT_d = dram_in(nc, "sgT", [H, P, T], BF16)
    h1 = dram_in(nc, "h1", [T, D], F32)
    w_out1 = dram_in(nc, "w_out1", [D, D], F32)
    fg = dram_in(nc, "fg", [1, D], F32)
    out = dram_out(nc, "out", [T, D], F32)

    cx = Ctx(nc)
    S = cx.S
    epst = cx.sb("epst", [P, 1], F32)
    S.do("pool", (lambda e: e.memset(epst[:, :], EPS)))
    og = cx.sb("og", [P, H, T], BF16)
    sg = cx.sb("sg", [P, H, T], BF16)
    og_res = [Res() for _ in range(H)]
    sg_res = [Res() for _ in range(H)]
    fg_bc = cx.sb("fg_bc", [P, D], F32)
    fg_res = Res()
    S.do("sp", (lambda e: e.dma_start(out=fg_bc[:, :], in_=fg[0:1, :].partition_broadcast(P))), writes=[fg_res], dma=True)
    blocks = [(w_out1, 0, db * WB) for db in range(D // WB)]
    ws = WStream(cx, blocks, nslots=3, nstage=3, ahead=2)
    for h in range(H):
        S.do("sp", (lambda e, h=h: e.dma_start(out=og[:, h, :], in_=oT_d[h, :, :])), writes=[og_res[h]], dma=True)
        S.do("sp", (lambda e, h=h: e.dma_start(out=sg[:, h, :], in_=sgT_d[h, :, :])), writes=[sg_res[h]], dma=True)
    for h in range(H):
        eng = "dve" if h % 2 == 0 else "pool"
        S.do(eng, (lambda e, h=h: e.tensor_tensor(out=og[:, h, :], in0=og[:, h, :], in1=sg[:, h, :], op=ALU.mult)),
             reads=[sg_res[h]], writes=[og_res[h]])
    mmb = Banks(cx, 4, "mm")
    h2 = cx.sb("h2", [P, NCH, D], F32)
    h2_res = [Res() for _ in range(NCH)]
    fss = cx.sb("fss", [P, NCH * 8], F32)
    junk = cx.sb("junk", [P, WB], BF16)
    xqring = Ring(cx, 3, "xq", [P, WB], F32)
    lastf = None
    for db in range(D // WB):
        slot, sres = ws.get(w_out1, 0, db * WB)
        for ch in range(NCH):
            xq, xqres = xqring.next()
            S.do("sp", (lambda e, xq=xq, ch=ch, db=db: e.dma_start(out=xq[:, :], in_=h1[ch * P:(ch + 1) * P, db * WB:(db + 1) * WB])),
                 writes=[xqres], dma=True)
            bank, bres = mmb.next()
            mm_group(S, bank[:, 0:WB], bres, [(og[:, h, ch * P:(ch + 1) * P], slot[:, h, :]) for h in range(H)],
                     reads=[sres] + og_res)
            hsl = h2[:, ch, db * WB:(db + 1) * WB]
            o = S.do("dve", (lambda e, bank=bank, hsl=hsl, xq=xq: e.tensor_tensor(out=hsl, in0=bank[:, 0:WB], in1=xq[:, :], op=ALU.add)),
                     reads=[bres, xqres], writes=[h2_res[ch]])
            acc = fss[:, ch * 8 + db:ch * 8 + db + 1]
            lastf = S.do("act", (lambda e, hsl=hsl, acc=acc: e.activation(out=junk[:, :], in_=hsl, func=AF.Square, accum_out=acc)), deps=[o])
    st = cx.sb("st", [P, 3 * NCH], F32)
    o1 = None
    for ch in range(NCH):
        o1 = S.do("dve", (lambda e, ch=ch: e.reduce_sum(out=st[:, ch:ch + 1], in_=fss[:, ch * 8:(ch + 1) * 8], axis=AX.X)), deps=[lastf])
    o2 = S.do("act", (lambda e: e.activation(out=st[:, NCH:2 * NCH], in_=st[:, 0:NCH], func=AF.Sqrt, scale=1.0 / D, bias=epst[:, 0:1])), deps=[o1])
    o3 = S.do("dve", (lambda e: e.reciprocal(out=st[:, 2 * NCH:3 * NCH], in_=st[:, NCH:2 * NCH])), deps=[o2])
    oring = Ring(cx, 2, "ofin", [P, D], F32)
    for ch in range(NCH):
        ot, ores = oring.next()
        S.do("dve", (lambda e, ot=ot, ch=ch: e.scalar_tensor_tensor(out=ot[:, :], in0=h2[:, ch, :], scalar=st[:, 2 * NCH + ch:2 * NCH + ch + 1],
                                                                     in1=fg_bc[:, :], op0=ALU.mult, op1=ALU.mult)),
             reads=[h2_res[ch], fg_res], writes=[ores], deps=[o3])
        S.do("sp", (lambda e, ot=ot, ch=ch: e.dma_start(out=out[ch * P:(ch + 1) * P, :], in_=ot[:, :])), reads=[ores], dma=True)
    cx.finish()
    return nc


_CACHE = {}


def _prog(name, fn):
    if name not in _CACHE:
        _CACHE[name] = fn()
    return _CACHE[name]


def host_consts():
    ident = np.eye(P, dtype=np.float32)
    s = np.arange(P)[:, None]
    t = np.arange(P)[None, :]
    cmask = (t >= s).astype(np.float32)
    negU = -(s >= t).astype(np.float32)
    negR = -(s < t).astype(np.float32)
    strict = (s < t).astype(np.float32)
    tri = np.concatenate([negU, negR, strict], axis=1)
    return ident, cmask, tri


def run_A(inputs):
    x = np.asarray(inputs["x"], dtype=np.float32)
    ident, cmask, tri = host_consts()
    norm_g = np.asarray(inputs["norm_g"], np.float32)
    w_in0 = np.ascontiguousarray(np.asarray(inputs["a_w_in"], np.float32)[0])
    vng = np.ascontiguousarray(np.asarray(inputs["a_v_norm_g"], np.float32)[0].reshape(E // P, P).T)
    wsT = np.ascontiguousarray(np.transpose(np.asarray(inputs["a_w_s"], np.float32)[0], (2, 0, 1)).reshape(P, G * P))
    bs = np.ascontiguousarray(np.asarray(inputs["a_b_s"], np.float32)[0].reshape(1, G * P))
    w_out0 = np.ascontiguousarray(np.asarray(inputs["a_w_out"], np.float32)[0])
    w_in1 = np.ascontiguousarray(np.asarray(inputs["b_w_in"], np.float32)[0])
    in_maps = []
    for c in range(NCORES):
        b, j = divmod(c, 4)
        in_maps.append({
            "x": np.ascontiguousarray(x[b, j * T:(j + 1) * T, :]),
            "g0": np.ascontiguousarray(norm_g[0:1]), "g1": np.ascontiguousarray(norm_g[1:2]),
            "w_in0": w_in0, "vng": vng, "wsT": wsT, "bs": bs, "cmask": cmask, "ident": ident,
            "w_out0": w_out0, "w_in1": w_in1,
        })
    nc = _prog("A", build_A)
    res = run_bass_kernel_spmd(nc, in_maps, core_ids=list(range(NCORES)))
    return res.results


def run_B(resA):
    ident, cmask, tri = host_consts()
    in_maps = []
    for c in range(NCORES):
        b, j = divmod(c, 4)
        qT = np.concatenate([np.asarray(resA[4 * b + r]["qT"])[4 * j:4 * j + 4] for r in range(4)], axis=2)
        kT = np.concatenate([np.asarray(resA[4 * b + r]["kT"])[4 * j:4 * j + 4] for r in range(4)], axis=2)
        vv = np.concatenate([np.asarray(resA[4 * b + r]["vv"])[4 * j:4 * j + 4] for r in range(4)], axis=2)
        in_maps.append({"qT": np.ascontiguousarray(qT), "kT": np.ascontiguousarray(kT), "vv": np.ascontiguousarray(vv), "tri": tri})
    nc = _prog("B", build_B)
    res = run_bass_kernel_spmd(nc, in_maps, core_ids=list(range(NCORES)))
    return res.results


def run_C(inputs, resA, resB):
    w_out1 = np.ascontiguousarray(np.asarray(inputs["b_w_out"], np.float32)[0])
    fg = np.ascontiguousarray(np.asarray(inputs["final_g"], np.float32).reshape(1, D))
    in_maps = []
    for c in range(NCORES):
        b, j = divmod(c, 4)
        oT = np.concatenate([np.asarray(resB[4 * b + r]["oT"])[:, :, j * T:(j + 1) * T] for r in range(4)], axis=0)
        in_maps.append({"oT": np.ascontiguousarray(oT), "sgT": np.asarray(resA[c]["sgT"]), "h1": np.asarray(resA[c]["h1"]),
                        "w_out1": w_out1, "fg": fg})
    nc = _prog("C", build_C)
    res = run_bass_kernel_spmd(nc, in_maps, core_ids=list(range(NCORES)))
    return res.results


def kernel(**inputs):
    resA = run_A(inputs)
    resB = run_B(resA)
    resC = run_C(inputs, resA, resB)
    out = np.zeros((2, S, D), dtype=np.float32)
    for c in range(NCORES):
        b, j = divmod(c, 4)
        out[b, j * T:(j + 1) * T, :] = np.asarray(resC[c]["out"], dtype=np.float32)
    return out
```
